# Optimizing a Trainium2 kernel written in Bass

```python
import math
import jax, jax.numpy as jnp
from jax import lax
import numpy as np

D_MODEL = 1024
BATCH = 8
SEQ = 2048
DEPTH = 4
DEC_BATCH = 128
DEC_SEQ = 8
PAST_LEN = 16384
PAGE_SIZE = 128

N_MIXERS = 2
POOL_WINDOWS = (2, 4, 8, 16)
POOL_GROUPS = len(POOL_WINDOWS)
POOL_GW = D_MODEL // POOL_GROUPS
POOL_MAXW = max(POOL_WINDOWS)
POOL_BUF = POOL_MAXW - 1
GLA_HEADS = 4
GLA_DK = D_MODEL // 2
GLA_DV = D_MODEL
GLA_DKH = GLA_DK // GLA_HEADS
GLA_DVH = GLA_DV // GLA_HEADS
GLA_GATE_RANK = 16
GLA_GATE_TEMP = 16.0
GLA_CHUNK = 32
GLA_IN = 2 * GLA_DK + 2 * GLA_DV + GLA_GATE_RANK
D_FF = ((8 * D_MODEL + 3 * 256 - 1) // (3 * 256)) * 256
N_POOL_LAYERS = (DEPTH + 1) // 2
N_GLA_LAYERS = DEPTH // 2
EPS = 1e-6

kernel_name = "pool_gla_hybrid_decoder_step"


def rmsnorm(x, g):
    xf = x.astype(jnp.float32)
    y = xf * lax.rsqrt(jnp.mean(xf * xf, axis=-1, keepdims=True) + EPS)
    return y.astype(x.dtype) * g


def swiglu(h, w1, w3, w2):
    return (jax.nn.silu(h @ w1) * (h @ w3)) @ w2


def pool_mixer(u, ctx, pos, w_pool, scale):
    B, T, D = u.shape
    ext = jnp.concatenate([jnp.zeros((B, 1, D), u.dtype), ctx.astype(u.dtype), u], axis=1)
    cs = jnp.cumsum(ext.astype(jnp.float32), axis=1)
    lag = jnp.concatenate(
        [cs[:, POOL_MAXW - w:POOL_MAXW - w + T, g * POOL_GW:(g + 1) * POOL_GW]
         for g, w in enumerate(POOL_WINDOWS)], axis=-1)
    sums = (cs[:, POOL_MAXW:] - lag).reshape(B, T, POOL_GROUPS, POOL_GW)
    wins = jnp.array(POOL_WINDOWS, dtype=jnp.int32)
    counts = jnp.minimum(pos[:, None] + 1, wins[None, :]).astype(jnp.float32)
    pooled = sums / counts[None, :, :, None]
    diff = (pooled - u.astype(jnp.float32).reshape(B, T, POOL_GROUPS, POOL_GW)).astype(u.dtype)
    y = jnp.einsum('btgc,gcd->btgd', diff, w_pool).reshape(B, T, D) * scale
    new_ctx = ext[:, -POOL_BUF:]
    return y, new_ctx


def gla_recurrence(q, k, v, log_a, S0):
    B, T, H, _ = q.shape
    C = min(GLA_CHUNK, T)
    n = -(-T // C)
    pad = n * C - T

    def prep(t):
        t = jnp.pad(t.astype(jnp.float32), ((0, 0), (0, pad), (0, 0), (0, 0)))
        return t.reshape(B, n, C, H, t.shape[-1]).transpose(1, 0, 3, 2, 4)

    qs, ks, vs, gs = prep(q), prep(k), prep(v), prep(log_a)
    mask = jnp.tril(jnp.ones((C, C), dtype=bool))[:, :, None]

    def step(S, inp):
        qc, kc, vc, gc = inp
        b = jnp.cumsum(gc, axis=2)
        diff = b[:, :, :, None, :] - b[:, :, None, :, :]
        decay = jnp.exp(jnp.where(mask, diff, -jnp.inf))
        A = jnp.einsum('bhid,bhjd,bhijd->bhij', qc, kc, decay)
        o = jnp.einsum('bhij,bhjv->bhiv', A, vc) + jnp.einsum('bhid,bhdv->bhiv', qc * jnp.exp(b), S)
        b_last = b[:, :, -1:, :]
        S_new = jnp.exp(b_last[:, :, 0, :, None]) * S + jnp.einsum('bhjd,bhjv->bhdv', kc * jnp.exp(b_last - b), vc)
        return S_new, o

    S_fin, os_ = lax.scan(step, S0.astype(jnp.float32), (qs, ks, vs, gs))
    o = os_.transpose(1, 0, 3, 2, 4).reshape(B, n * C, H, -1)[:, :T]
    return o, S_fin


def gla_mixer(u, S0, w_in, w_a2, b_a, norm_w, w_o):
    B, T, _ = u.shape
    proj = u @ w_in
    q = proj[..., :GLA_DK].reshape(B, T, GLA_HEADS, GLA_DKH) * (GLA_DKH ** -0.5)
    k = proj[..., GLA_DK:2 * GLA_DK].reshape(B, T, GLA_HEADS, GLA_DKH)
    v = proj[..., 2 * GLA_DK:2 * GLA_DK + GLA_DV].reshape(B, T, GLA_HEADS, GLA_DVH)
    r = proj[..., 2 * GLA_DK + GLA_DV:2 * GLA_DK + 2 * GLA_DV]
    a_low = proj[..., 2 * GLA_DK + 2 * GLA_DV:]
    log_a = (jax.nn.log_sigmoid((a_low @ w_a2 + b_a).astype(jnp.float32)) / GLA_GATE_TEMP)
    log_a = log_a.reshape(B, T, GLA_HEADS, GLA_DKH)
    o, S = gla_recurrence(q, k, v, log_a, S0)
    o = o * lax.rsqrt(jnp.mean(o * o, axis=-1, keepdims=True) + EPS)
    o = o.astype(u.dtype) * norm_w.reshape(GLA_HEADS, GLA_DVH)
    o = o.reshape(B, T, GLA_DV) * jax.nn.silu(r)
    return o @ w_o, S.astype(S0.dtype)


def trunk(x, pool_ctx, gla_S, pos, norm_mix, norm_ffn, norm_final, pool_w, pool_scale,
          gla_w_in, gla_w_a2, gla_b_a, gla_norm, gla_w_o, ffn_w1, ffn_w3, ffn_w2):
    new_pool, new_gla = [], []
    for i in range(DEPTH):
        h = rmsnorm(x, norm_mix[i])
        j = i // N_MIXERS
        if i % N_MIXERS == 0:
            y, c = pool_mixer(h, pool_ctx[j], pos, pool_w[j], pool_scale[j])
            new_pool.append(c)
        else:
            y, S = gla_mixer(h, gla_S[j], gla_w_in[j], gla_w_a2[j], gla_b_a[j], gla_norm[j], gla_w_o[j])
            new_gla.append(S)
        x = x + y
        x = x + swiglu(rmsnorm(x, norm_ffn[i]), ffn_w1[i], ffn_w3[i], ffn_w2[i])
    return rmsnorm(x, norm_final), jnp.stack(new_pool), jnp.stack(new_gla)


def setup_inputs(seed: int = 0) -> dict:
    key = jax.random.key(seed)
    ks = jax.random.split(key, 20)
    f32 = jnp.float32
    nrm = lambda k, shape, s: jax.random.normal(k, shape, f32) * s
    return {
        "x_prompt": nrm(ks[0], (BATCH, SEQ, D_MODEL), 1.0),
        "x_sample": nrm(ks[1], (DEC_BATCH, DEC_SEQ, D_MODEL), 1.0),
        "state_pool": nrm(ks[2], (N_POOL_LAYERS, DEC_BATCH, POOL_BUF, D_MODEL), 1.0),
        "state_gla": nrm(ks[3], (N_GLA_LAYERS, DEC_BATCH, GLA_HEADS, GLA_DKH, GLA_DVH), 0.3),
        "norm_mix": 1.0 + nrm(ks[4], (DEPTH, D_MODEL), 0.02),
        "norm_ffn": 1.0 + nrm(ks[5], (DEPTH, D_MODEL), 0.02),
        "norm_final": 1.0 + nrm(ks[6], (D_MODEL,), 0.02),
        "pool_w": nrm(ks[7], (N_POOL_LAYERS, POOL_GROUPS, POOL_GW, POOL_GW), POOL_GW ** -0.5),
        "pool_scale": 1.0 + nrm(ks[8], (N_POOL_LAYERS, D_MODEL), 0.1),
        "gla_w_in": nrm(ks[9], (N_GLA_LAYERS, D_MODEL, GLA_IN), D_MODEL ** -0.5),
        "gla_w_a2": nrm(ks[10], (N_GLA_LAYERS, GLA_GATE_RANK, GLA_DK), GLA_GATE_RANK ** -0.5),
        "gla_b_a": nrm(ks[11], (N_GLA_LAYERS, GLA_DK), 0.1),
        "gla_norm": 1.0 + nrm(ks[12], (N_GLA_LAYERS, GLA_DV), 0.02),
        "gla_w_o": nrm(ks[13], (N_GLA_LAYERS, GLA_DV, D_MODEL), GLA_DV ** -0.5),
        "ffn_w1": nrm(ks[14], (DEPTH, D_MODEL, D_FF), D_MODEL ** -0.5),
        "ffn_w3": nrm(ks[15], (DEPTH, D_MODEL, D_FF), D_MODEL ** -0.5),
        "ffn_w2": nrm(ks[16], (DEPTH, D_FF, D_MODEL), D_FF ** -0.5),
    }


def reference(x_prompt, x_sample, state_pool, state_gla, norm_mix, norm_ffn, norm_final, pool_w, pool_scale,
              gla_w_in, gla_w_a2, gla_b_a, gla_norm, gla_w_o, ffn_w1, ffn_w3, ffn_w2):
    B, T = x_prompt.shape[0], x_prompt.shape[1]
    Bs, Ts = x_sample.shape[0], x_sample.shape[1]
    pos_prompt = jnp.arange(T, dtype=jnp.int32)
    pos_sample = PAST_LEN + jnp.arange(Ts, dtype=jnp.int32)
    pool0 = jnp.zeros((N_POOL_LAYERS, B, POOL_BUF, D_MODEL), x_prompt.dtype)
    gla0 = jnp.zeros((N_GLA_LAYERS, B, GLA_HEADS, GLA_DKH, GLA_DVH), jnp.float32)
    y_prompt, new_pool_prompt, new_gla_prompt = trunk(
        x_prompt, pool0, gla0, pos_prompt, norm_mix, norm_ffn, norm_final, pool_w, pool_scale,
        gla_w_in, gla_w_a2, gla_b_a, gla_norm, gla_w_o, ffn_w1, ffn_w3, ffn_w2)
    y_sample, new_pool_sample, new_gla_sample = trunk(
        x_sample, state_pool, state_gla, pos_sample, norm_mix, norm_ffn, norm_final, pool_w, pool_scale,
        gla_w_in, gla_w_a2, gla_b_a, gla_norm, gla_w_o, ffn_w1, ffn_w3, ffn_w2)
    return (y_prompt, y_sample, new_pool_prompt, new_gla_prompt, new_pool_sample, new_gla_sample)
```

```python
import contextlib
import numpy as np
import concourse.bass as bass
import concourse.mybir as mybir
from concourse.bass_utils import run_bass_kernel_spmd

F32 = mybir.dt.float32
BF16 = mybir.dt.bfloat16
AF = mybir.ActivationFunctionType
ALU = mybir.AluOpType

D = 1024
NC8 = 8
NCH = 8
SEQ = 2048
NSS = 16
TS = 8
NT = SEQ + NSS * TS
NTILE = NT // 128
DFF = 2816
NFC = DFF // 128
GIN = 3088
EPS = 1e-6
STS = [(0, 512), (512, 512), (1024, 512), (1536, 512), (2048, 128)]
FFN_GROUPS = [(0, 4), (4, 4), (8, 4), (12, 4), (16, 3), (19, 3)]
SLOT_ELEMS = 12288
CELL = 128

V_MIX = 0
V_FFN = 32
V_FIN = 64
V_PSC = 72
V_GNW = 88
NVEC = 104
C_ID = 0
C_TRI_P = 128
C_TRI_S = 256
C_MASK_P = 384
C_MASK_S = 512
C_RMASK = 640
C_IC = 656
C_IC2 = 716
NCONST = 776


class Ins:
    __slots__ = ("eng", "fn", "deps", "idx", "flag", "semval", "is_dma", "dma_sem", "dma_val", "waits", "uid", "_prev", "_K")


class Prog:
    ENGS = ("pe", "act", "dve", "pool", "sp")

    def __init__(self, n_dma_sems=None):
        self.q = {e: [] for e in self.ENGS}
        self.cells = {}
        self.uid = 0
        self.n_dma_sems = n_dma_sems or {"sp": 20, "pool": 20, "act": 4}
        self.dma_count = {e: 0 for e in self.ENGS}
        self.dma_sem_cnt = {}

    @staticmethod
    def regions(ap):
        t = ap.tensor
        sp = str(ap.space) if hasattr(ap, "space") else ""
        if "DRAM" in sp.upper() or "HBM" in sp.upper():
            return None
        dims = list(ap.ap)
        dsz = mybir.dt.size(ap.dtype)
        row = dims[0][0]
        off = ap.offset
        f0 = off % row if row > 0 else off
        free = [(s, c) for (s, c) in dims[1:] if c > 1]
        name = t.name
        if not free:
            return name, [(f0 * dsz, (f0 + 1) * dsz)]
        free_sorted = free
        inner_s, inner_c = free_sorted[-1]
        outer = free_sorted[:-1]
        nouter = 1
        for s, c in outer:
            nouter *= c
        if inner_s != 1 or nouter > 64:
            hi = f0 + sum((c - 1) * abs(s) for s, c in free) + 1
            return name, [(f0 * dsz, hi * dsz)]
        offs = [f0]
        for s, c in outer:
            offs = [o + i * s for o in offs for i in range(c)]
        return name, [(o * dsz, (o + inner_c) * dsz) for o in offs]

    @staticmethod
    def _is_psum(ap):
        return ap.tensor.name.startswith("ps")

    def _cells(self, ap):
        r = self.regions(ap)
        if r is None:
            return []
        name, ranges = r
        if name.startswith("ps"):
            return [(name, 0)]
        out = []
        for lo, hi in ranges:
            for c in range(lo // CELL, (hi - 1) // CELL + 1):
                out.append((name, c))
        return out

    def add(self, eng, fn, reads=(), writes=(), is_dma=False, extra=()):
        ins = Ins()
        ins.eng = eng
        ins.fn = fn
        ins.deps = set(d for d in extra if d is not None)
        ins.flag = False
        ins.is_dma = is_dma
        ins.uid = self.uid
        self.uid += 1
        ins.waits = []
        rkey = ("dma", ins.uid) if is_dma else eng
        ps_aps = [ap for ap in list(reads) + list(writes) if self._is_psum(ap)]
        reads = [ap for ap in reads if not self._is_psum(ap)]
        writes = [ap for ap in writes if not self._is_psum(ap)]
        for ap in ps_aps:
            cell = (ap.tensor.name, 0)
            st = self.cells.get(cell)
            if st is None:
                st = self.cells[cell] = [None, {}]
            w = st[0]
            if w is not None and w is not ins and w.eng != eng:
                ins.deps.add(w)
            st[0] = ins
        for ap in reads:
            for cell in self._cells(ap):
                st = self.cells.get(cell)
                if st is None:
                    st = self.cells[cell] = [None, {}]
                w = st[0]
                if w is not None and w is not ins:
                    if not (w.eng == eng == "pe" and not w.is_dma and not is_dma):
                        ins.deps.add(w)
                st[1][rkey] = ins
        for ap in writes:
            for cell in self._cells(ap):
                st = self.cells.get(cell)
                if st is None:
                    st = self.cells[cell] = [None, {}]
                w = st[0]
                if w is not None and w is not ins:
                    if w.is_dma or is_dma or w.eng != eng or eng != "pe":
                        ins.deps.add(w)
                for k, r in st[1].items():
                    if r is ins:
                        continue
                    if r.is_dma or is_dma or r.eng != eng or eng != "pe":
                        ins.deps.add(r)
                st[0] = ins
                st[1] = {}
        ins.idx = len(self.q[eng])
        self.q[eng].append(ins)
        return ins

    def finalize(self, nc, stack):
        for e in self.ENGS:
            for ins in self.q[e]:
                for d in ins.deps:
                    if not d.is_dma:
                        d.flag = True
        self.esem = {}
        for e in self.ENGS:
            self.esem[e] = stack.enter_context(nc.semaphore("es_" + e))
            n = 0
            for ins in self.q[e]:
                if ins.flag and not ins.is_dma:
                    n += 1
                    ins.semval = n
        self.dsems = {}
        for e in self.ENGS:
            k = self.n_dma_sems.get(e, 0)
            self.dsems[e] = [stack.enter_context(nc.semaphore("ds_%s_%d" % (e, i))) for i in range(k)]
        cnt = {}
        for e in self.ENGS:
            i = 0
            for ins in self.q[e]:
                if ins.is_dma:
                    pool = self.dsems[e]
                    j = i % len(pool)
                    i += 1
                    key = (e, j)
                    c = cnt.get(key, 0)
                    ins.dma_sem = pool[j]
                    ins.dma_val = 16 * (c + 1)
                    cnt[key] = c + 1
                    ins._prev = (pool[j], 16 * c) if c > 0 else None
        self.final_dma = [(self.dsems[e][j], 16 * c) for (e, j), c in cnt.items()]
        allins = sorted((ins for e in self.ENGS for ins in self.q[e]), key=lambda i: i.uid)
        bysem = {}
        zero = {e: 0 for e in self.ENGS}
        known_eng = {e: dict(zero) for e in self.ENGS}
        waited_d = {e: {} for e in self.ENGS}
        own = {e: 0 for e in self.ENGS}
        for ins in allins:
            e = ins.eng
            known = known_eng[e]
            need_e = {}
            need_d = {}
            for d in ins.deps:
                if d.is_dma:
                    k = id(d.dma_sem)
                    if k not in need_d or need_d[k][1] < d.dma_val:
                        need_d[k] = (d.dma_sem, d.dma_val, d)
                else:
                    if d.eng not in need_e or need_e[d.eng].semval < d.semval:
                        need_e[d.eng] = d
            if ins.is_dma and ins._prev is not None:
                k = id(ins._prev[0])
                if k not in need_d or need_d[k][1] < ins._prev[1]:
                    need_d[k] = (ins._prev[0], ins._prev[1], None)
            for k, (sem, val, d) in need_d.items():
                if waited_d[e].get(k, 0) >= val:
                    continue
                waited_d[e][k] = val
                ins.waits.append((sem, val))
                if d is not None:
                    for ee, v in d._K.items():
                        if known[ee] < v:
                            known[ee] = v
            for d in sorted(need_e.values(), key=lambda x: -x.uid):
                if known[d.eng] >= d.semval:
                    continue
                ins.waits.append((self.esem[d.eng], d.semval))
                for ee, v in d._K.items():
                    if known[ee] < v:
                        known[ee] = v
                if known[d.eng] < d.semval:
                    known[d.eng] = d.semval
            if not ins.is_dma and ins.flag:
                own[e] = ins.semval
                K = dict(known)
                if K[e] < ins.semval:
                    K[e] = ins.semval
                ins._K = K
            else:
                ins._K = dict(known)

    def emit(self, engname, eng, final=False):
        esem = self.esem[engname]
        for ins in self.q[engname]:
            fuse = (not ins.is_dma and len(ins.waits) >= 1)
            for sem, val in (ins.waits[:-1] if fuse else ins.waits):
                eng.wait_ge(sem, val)
            bi = ins.fn(eng)
            if fuse:
                bi._wait_ge(*ins.waits[-1])
            if ins.is_dma:
                bi.then_inc(ins.dma_sem, 16)
            elif ins.flag:
                bi.then_inc(esem, 1)
        if final:
            for sem, val in self.final_dma:
                eng.wait_ge(sem, val)


def build_program():
    nc = bass.Bass("TRN2", target_bir_lowering=False)
    P = Prog()

    def din(name, shape):
        return nc.dram_tensor(name, list(shape), F32, kind="ExternalInput").ap()

    def dout(name, shape):
        return nc.dram_tensor(name, list(shape), F32, kind="ExternalOutput").ap()

    xin = din("xin", [NT, D])
    sp_in = din("sp_in", [2, NSS, 15, D])
    sg_in = din("sg_in", [2, NSS, 4, 128, 256])
    vecs_d = din("vecs", [128, NVEC])
    consts_d = din("consts", [128, NCONST])
    pool_w = din("pool_w", [2, 4, 256, 256])
    w_in = din("gla_w_in", [2, D, GIN])
    w_a2 = din("gla_w_a2", [2, 16, 512])
    b_a = din("gla_b_a", [2, 512])
    w_o = din("gla_w_o", [2, D, D])
    w1 = din("ffn_w1", [4, D, DFF])
    w3 = din("ffn_w3", [4, D, DFF])
    w2 = din("ffn_w2", [4, DFF, D])

    y_out = dout("y", [NT, D])
    npp = dout("npp", [2, 15, D])
    ngp = dout("ngp", [2, 4, 128, 256])
    nps = dout("nps", [2, NSS, 15, D])
    ngs = dout("ngs", [2, NSS, 4, 128, 256])

    stack = contextlib.ExitStack()
    with stack:
        def sb(name, shape, dt):
            return stack.enter_context(nc.sbuf_tensor(name, list(shape), dt))

        xT = sb("xT", [128, NCH, NT], F32)
        hT = sb("hT", [128, NCH, NT], BF16)
        slots = [sb("wslot%d" % i, [128, SLOT_ELEMS], BF16) for i in range(2)]
        vecs = sb("vecs_sb", [128, NVEC], F32)
        cst = sb("cst", [128, NCONST], F32)
        idb = sb("idb", [128, 128], BF16)
        onesb = sb("onesb", [128, 128], BF16)
        AL = sb("AL", [128, NT], BF16)
        AL_f = AL.bitcast(F32)
        assert AL_f.name == AL.name
        WA = sb("WA", [17, 512], BF16)
        WAL = sb("WAL", [128, 8, 16], BF16)
        ARENA_BYTES = 49408
        arena = sb("arena", [128, ARENA_BYTES // 4], F32)
        arena_b = arena.bitcast(BF16)
        assert arena_b.name == arena.name
        ps = [stack.enter_context(nc.psum_tensor("ps%d" % i, [128, 512], F32)) for i in range(8)]

        def af(off, n):
            assert off % 4 == 0
            return arena[:, off // 4: off // 4 + n]

        def ab(off, n):
            assert off % 2 == 0
            return arena_b[:, off // 2: off // 2 + n]

        def mm(out, lhsT, rhs, start=True, stop=True, extra=()):
            return P.add("pe", lambda e: e.matmul(out, lhsT, rhs, start=start, stop=stop),
                         reads=[lhsT, rhs], writes=[out], extra=extra)

        def act(out, in_, func, bias=0.0, scale=1.0):
            reads = [in_]
            if not isinstance(bias, (int, float)):
                reads.append(bias)
            if not isinstance(scale, (int, float)):
                reads.append(scale)
            return P.add("act", lambda e: e.activation(out, in_, func, bias=bias, scale=scale),
                         reads=reads, writes=[out])

        def tt(eng, out, in0, in1, op):
            return P.add(eng, lambda e: e.tensor_tensor(out, in0, in1, op), reads=[in0, in1], writes=[out])

        def ts(eng, out, in0, s1, s2, op0, op1=None):
            reads = [in0] + [s for s in (s1, s2) if s is not None and not isinstance(s, (int, float))]
            if op1 is None:
                return P.add(eng, lambda e: e.tensor_single_scalar(out, in0, s1, op0), reads=reads, writes=[out])
            return P.add(eng, lambda e: e.tensor_scalar(out, in0, s1, s2, op0, op1), reads=reads, writes=[out])

        def stt(eng, out, in0, scalar, in1, op0, op1):
            reads = [in0, in1] + ([] if isinstance(scalar, (int, float)) else [scalar])
            return P.add(eng, lambda e: e.scalar_tensor_tensor(out, in0, scalar, in1, op0, op1),
                         reads=reads, writes=[out])

        def cp(eng, out, in_):
            if eng == "act":
                return act(out, in_, AF.Copy)
            return P.add(eng, lambda e: e.tensor_copy(out, in_), reads=[in_], writes=[out])

        def memset(eng, out, val):
            return P.add(eng, lambda e: e.memset(out, val), writes=[out])

        def dma(q, out, in_):
            return P.add(q, lambda e: e.dma_start(out=out, in_=in_), reads=[in_], writes=[out], is_dma=True)

        def vcol(base, idx):
            return vecs[:, base + idx: base + idx + 1]

        dma("sp", vecs[:, :], vecs_d)
        dma("sp", cst[:, :], consts_d)
        cp("dve", idb[:, :], cst[:, C_ID:C_ID + 128])
        memset("dve", onesb[:, :], 1.0)
        ident = cst[:, C_ID:C_ID + 128]

        wstate = {"n": 0}

        def next_slot():
            s = slots[wstate["n"] % 2]
            wstate["n"] += 1
            return s

        def load_ffn_group(l, c0, g):
            s = next_slot()
            W1 = s[:, 0:8 * g * 128].rearrange("p (k n) -> p k n", k=8)
            W3 = s[:, 4096:4096 + 8 * g * 128].rearrange("p (k n) -> p k n", k=8)
            W2 = s[:, 8192:8192 + g * 1024].rearrange("p (k n) -> p k n", k=g)
            dma("pool", W1, w1[l].rearrange("(k p) n -> p k n", p=128)[:, :, c0 * 128:(c0 + g) * 128])
            dma("pool", W3, w3[l].rearrange("(k p) n -> p k n", p=128)[:, :, c0 * 128:(c0 + g) * 128])
            dma("pool", W2, w2[l][c0 * 128:(c0 + g) * 128, :].rearrange("(k p) n -> p k n", p=128))
            return W1, W3, W2

        def load_gla_head(j, h):
            s = next_slot()
            WQ = s[:, 0:1024].rearrange("p (k n) -> p k n", k=8)
            WK = s[:, 1024:2048].rearrange("p (k n) -> p k n", k=8)
            WV = s[:, 2048:4096].rearrange("p (k n) -> p k n", k=8)
            WR = s[:, 4096:6144].rearrange("p (k n) -> p k n", k=8)
            WO = s[:, 6144:8192].rearrange("p (k n) -> p k n", k=2)
            wi = w_in[j].rearrange("(k p) n -> p k n", p=128)
            dma("pool", WQ, wi[:, :, h * 128:(h + 1) * 128])
            dma("pool", WK, wi[:, :, 512 + h * 128:512 + (h + 1) * 128])
            dma("pool", WV, wi[:, :, 1024 + h * 256:1024 + (h + 1) * 256])
            dma("pool", WR, wi[:, :, 2048 + h * 256:2048 + (h + 1) * 256])
            dma("pool", WO, w_o[j][h * 256:(h + 1) * 256, :].rearrange("(k p) n -> p k n", p=128))
            return WQ, WK, WV, WR, WO

        A_SQ = 0
        A_RSTD = 2048
        A_PH = 6144
        FB = 6144
        PB0 = 22528
        sqbuf = [ab(A_SQ + i * 1024, 512) for i in range(2)]
        rstd = [af(A_RSTD + i * 2048, 512) for i in range(2)]
        cnt = {"sq": 0, "rs": 0, "ps": 0}

        sched = []
        keyc = {"base": 0.0, "n": 0}

        def push_stage(items, step=2.0, base=None, advance=5.0):
            if base is None:
                base = keyc["base"]
            for i, f in enumerate(items):
                sched.append((base + step * i, keyc["n"], f))
                keyc["n"] += 1
            keyc["base"] = base + advance
            return base

        def norm_stats(si, psb):
            t0, n = STS[si]
            for c in range(NCH):
                sq = sqbuf[cnt["sq"] % 2]
                cnt["sq"] += 1
                act(sq[:, 0:n], xT[:, c, t0:t0 + n], AF.Square, scale=1.0 / 32.0)
                mm(psb[:, 0:n], onesb[:, :], sq[:, 0:n], start=(c == 0), stop=(c == NCH - 1))
            rs = rstd[cnt["rs"] % 2]
            cnt["rs"] += 1
            act(rs[:, 0:n], psb[:, 0:n], AF.Ln, bias=EPS)
            act(rs[:, 0:n], rs[:, 0:n], AF.Exp, scale=-0.5)
            return rs

        def norm_apply(si, rs, gbase):
            t0, n = STS[si]
            for c in range(NCH):
                stt("dve", hT[:, c, t0:t0 + n], xT[:, c, t0:t0 + n], vcol(gbase, c), rs[:, 0:n], ALU.mult, ALU.mult)

        iobuf = [af(FB + i * 4096, 1024) for i in range(2)]
        ihi = [ab(FB + 8192 + i * 4096, 1024) for i in range(2)]
        ilo = [ab(FB + 8192 + i * 4096 + 2048, 1024) for i in range(2)]

        def input_item(si):
            def f():
                t0, n = STS[si]
                for t in range(t0 // 128, (t0 + n) // 128):
                    io = iobuf[t % 2]
                    hi, lo = ihi[t % 2], ilo[t % 2]
                    dma("sp", io, xin[t * 128:(t + 1) * 128, :])
                    cp("act", hi[:, :], io[:, :])
                    tt("dve", lo[:, :], io[:, :], hi[:, :], ALU.subtract)
                    for half in range(2):
                        pb = ps[(2 * t + half) % 4]
                        for cc in range(4):
                            c = half * 4 + cc
                            mm(pb[:, cc * 128:(cc + 1) * 128], hi[:, c * 128:(c + 1) * 128], idb[:, :],
                               start=True, stop=False)
                            mm(pb[:, cc * 128:(cc + 1) * 128], lo[:, c * 128:(c + 1) * 128], idb[:, :],
                               start=False, stop=True)
                        cp("act" if half == 0 else "dve", xT[:, half * 4:half * 4 + 4, t * 128:(t + 1) * 128],
                           pb[:, :].rearrange("p (c n) -> p c n", c=4))
            return f

        def pool_items(l):
            j = l // 2
            EB = [ab(PB0 + i * 1152, 527) for i in range(4)]
            U15 = af(PB0 + 4608, 15)
            T15 = af(PB0 + 4736, 15)
            HAL = ab(PB0 + 4864, 120).rearrange("p (c n) -> p c n", c=8)
            UP = af(PB0 + 5120, 120).rearrange("p (c n) -> p c n", c=8)
            ES = af(PB0 + 5632, 368).rearrange("p (b n) -> p b n", b=16)
            SA2 = af(PB0 + 7104, 368).rearrange("p (b n) -> p b n", b=16)
            SB2 = af(PB0 + 8576, 368).rearrange("p (b n) -> p b n", b=16)
            USc = af(PB0 + 10048, 128)
            STG = [af(PB0 + 10624, 1024), AL_f[:, 0:1024]]
            PW = ab(PB0 + 14720, 2048).rearrange("p (g k n) -> p g k n", g=4, k=2)
            dcoef = ab(PB0 + 18816, 1024).rearrange("p (i n) -> p i n", i=8)
            assert PB0 + 18816 + 2048 <= ARENA_BYTES

            def build_dcoef():
                for g_ in range(4):
                    w_ = 2 << g_
                    for i_, val in ((0, 1.0 / w_ - 1.0), (1, 1.0 / w_)):
                        dst = dcoef[:, 2 * g_ + i_, :]
                        P.add("dve", (lambda e, dst=dst, val=val: e.tensor_single_scalar(dst, ident, val, ALU.mult)),
                              reads=[ident], writes=[dst])

            ebn = {"n": 0}

            def pool_mm(si):
                t0, n = STS[si]
                for g in range(4):
                    for m in range(2):
                        pb = ps[(2 * g + m) % 4]
                        for k in range(2):
                            mm(pb[:, 0:n], PW[:, g, k, m * 128:(m + 1) * 128], hT[:, 2 * g + k, t0:t0 + n],
                               start=(k == 0), stop=(k == 1))
                        c = 2 * g + m
                        stt("dve", xT[:, c, t0:t0 + n], pb[:, 0:n], vcol(V_PSC + j * 8, c), xT[:, c, t0:t0 + n],
                            ALU.mult, ALU.add)

            def prompt_item(si):
                def f():
                    t0, n = STS[si]
                    if si == 0:
                        build_dcoef()
                        dma("pool", PW[:, :, :, :], pool_w[j].rearrange("g (k p) n -> p g k n", p=128))
                        for part, (r0, nr) in enumerate([(0, 128), (128, 112)]):
                            dma("sp", STG[part][0:nr, :], sp_in[j].rearrange("b r d -> (b r) d")[r0:r0 + nr, :])
                        dma("sp", nps[j][:, 0:7, :], sp_in[j][:, 8:15, :])
                    rs = norm_stats(si, ps[6 + (si % 2)])
                    L = 15 + n
                    for c in range(NCH):
                        g = c // 2
                        w = 2 << g
                        Eb = EB[ebn["n"] % 4]
                        ebn["n"] += 1
                        if si == 0:
                            memset("pool", Eb[:, 0:15], 0.0)
                        else:
                            cp("pool", Eb[:, 0:15], HAL[:, c, :])
                        stt("dve", Eb[:, 15:L], xT[:, c, t0:t0 + n], vcol(V_MIX + l * 8, c), rs[:, 0:n],
                            ALU.mult, ALU.mult)
                        cp("pool", HAL[:, c, :], Eb[:, 512:527])
                        if si == 3:
                            stt("dve", UP[:, c, :], xT[:, c, t0 + n - 15:t0 + n], vcol(V_MIX + l * 8, c),
                                rs[:, n - 15:n], ALU.mult, ALU.mult)
                        pb = ps[c % 4]
                        for k in range(w):
                            mm(pb[:, 0:n], dcoef[:, 2 * g + (0 if k == 0 else 1), :], Eb[:, 15 - k:15 - k + n],
                               start=(k == 0), stop=(k == w - 1))
                        cp("act", hT[:, c, t0:t0 + n], pb[:, 0:n])
                        if si == 0:
                            stt("dve", U15[:, :], xT[:, c, 0:15], vcol(V_MIX + l * 8, c), rs[:, 0:15],
                                ALU.mult, ALU.mult)
                            tt("dve", T15[:, :], pb[:, 0:15], U15[:, :], ALU.add)
                            tt("dve", T15[:, :], T15[:, :], cst[:, C_IC2 + g * 15:C_IC2 + (g + 1) * 15], ALU.mult)
                            tt("dve", hT[:, c, 0:15], T15[:, :], U15[:, :], ALU.subtract)
                    pool_mm(si)
                return f

            def sample_item():
                si = len(STS) - 1
                t0, n = STS[si]
                rs = norm_stats(si, ps[6 + (si % 2)])
                for c in range(NCH):
                    g = c // 2
                    w = 2 << g
                    pc = ps[c % 2]
                    for part, (r0, nr) in enumerate([(0, 128), (128, 112)]):
                        mm(pc[:, r0:r0 + nr], STG[part][0:nr, c * 128:(c + 1) * 128], ident[0:nr, 0:nr])
                    cp("act", ES[:, :, 0:15], pc[:, 0:240].rearrange("p (b r) -> p b r", b=16))
                    stt("dve", USc[:, :], xT[:, c, t0:t0 + n], vcol(V_MIX + l * 8, c), rs[:, 0:n],
                        ALU.mult, ALU.mult)
                    cp("pool", ES[:, :, 15:23], USc[:, :].rearrange("p (b t) -> p b t", b=16))
                    mm(ps[4 + c // 4][:, (c % 4) * 128:(c % 4 + 1) * 128], USc[:, :], ident)
                    src = ES
                    bufs = [SA2, SB2]
                    sh = 1
                    k = 0
                    while sh < w:
                        dst = bufs[k % 2]
                        lo = 2 * sh - 1
                        tt("dve", dst[:, :, lo:23], src[:, :, lo:23], src[:, :, lo - sh:23 - sh], ALU.add)
                        src = dst
                        sh *= 2
                        k += 1
                    stt("dve", hT[:, c, t0:t0 + n].rearrange("p (b t) -> p b t", b=16), src[:, :, 15:23],
                        1.0 / w, ES[:, :, 15:23], ALU.mult, ALU.subtract)
                pool_mm(si)
                for half in range(2):
                    cp("act", STG[0][:, half * 512:(half + 1) * 512], ps[4 + half][:, :])
                for b in range(NSS):
                    dma("sp", nps[j][b, 7:15, :], STG[0][b * 8:(b + 1) * 8, :])
                for half in range(2):
                    pb = ps[4 + half]
                    for cc in range(4):
                        c = half * 4 + cc
                        mm(pb[0:15, cc * 128:(cc + 1) * 128], UP[:, c, :], ident)
                    cp("act", STG[1][0:15, half * 512:(half + 1) * 512], pb[0:15, :])
                dma("sp", npp[j], STG[1][0:15, :])
            return [prompt_item(si) for si in range(4)] + [sample_item]

        gbufs = [ab(FB + i * 4096, 2048).rearrange("p (j n) -> p j n", j=4) for i in range(3)]
        gfree = list(range(3))
        sabuf = [af(FB + 12288 + i * 2048, 512) for i in range(2)]
        ksa = {"n": 0}

        def ffn_norm_items(l):
            def mk(si):
                def f():
                    rs = norm_stats(si, ps[6 + (si % 2)])
                    norm_apply(si, rs, V_FFN + l * 8)
                return f
            return [mk(si) for si in range(len(STS))]

        def ffn_group_items(l, c0, g, with_norm=False):
            st = {}

            def ab_stage(si):
                t0, n = STS[si]
                gi = gfree.pop(0)
                gb = gbufs[gi]
                W1, W3, W2 = st["w"]
                for jj in range(g):
                    pa = ps[jj % 2]
                    pbb = ps[2 + jj % 2]
                    for k in range(NCH):
                        mm(pbb[:, 0:n], W3[:, k, jj * 128:(jj + 1) * 128], hT[:, k, t0:t0 + n],
                           start=(k == 0), stop=(k == NCH - 1))
                    for k in range(NCH):
                        mm(pa[:, 0:n], W1[:, k, jj * 128:(jj + 1) * 128], hT[:, k, t0:t0 + n],
                           start=(k == 0), stop=(k == NCH - 1))
                    sa = sabuf[ksa["n"] % 2]
                    ksa["n"] += 1
                    act(sa[:, 0:n], pa[:, 0:n], AF.Silu)
                    tt("dve", gb[:, jj, 0:n], sa[:, 0:n], pbb[:, 0:n], ALU.mult)
                return gi

            def y_stage(si, gi):
                t0, n = STS[si]
                gb = gbufs[gi]
                W1, W3, W2 = st["w"]
                accs = []
                for m in range(NCH):
                    py = ps[4 + m % 4]
                    ex = (accs[m - 2],) if (m % 2 == 0 and m >= 2) else ()
                    for jj in range(g):
                        mm(py[:, 0:n], W2[:, jj, m * 128:(m + 1) * 128], gb[:, jj, 0:n],
                           start=(jj == 0), stop=(jj == g - 1), extra=(ex if jj == 0 else ()))
                    accs.append(tt("dve", xT[:, m, t0:t0 + n], py[:, 0:n], xT[:, m, t0:t0 + n], ALU.add))
                gfree.append(gi)

            def mk(si):
                def f():
                    if si == 0:
                        st["w"] = load_ffn_group(l, c0, g)
                    if with_norm:
                        rs = norm_stats(si, ps[6 + (si % 2)])
                        norm_apply(si, rs, V_FFN + l * 8)
                    gi = ab_stage(si)
                    if si > 0:
                        y_stage(si - 1, st["pend"])
                    st["pend"] = gi
                    if si == len(STS) - 1:
                        y_stage(si, gi)
                return f
            return [mk(si) for si in range(len(STS))]

        def gla_layer(l):
            j = l // 2
            o = [A_PH]

            def alloc(nbytes):
                r = o[0]
                o[0] += (nbytes + 255) // 256 * 256
                return r
            qf = af(alloc(2048), 512)
            kf = af(alloc(2048), 512)
            sr = ab(alloc(2048), 1024).rearrange("p (c n) -> p c n", c=2)
            vt = ab(alloc(2048), 1024).rearrange("p (t n) -> p t n", t=4)
            og = ab(alloc(2048), 1024).rearrange("p (c n) -> p c n", c=2)
            o_e1 = alloc(2048)
            e1 = af(o_e1, 512)
            Ep = af(alloc(2048), 512)
            Em = af(alloc(2048), 512)
            qt = ab(alloc(1024), 512)
            kt = ab(alloc(1024), 512)
            kh = ab(alloc(1024), 512)
            khtok = ab(alloc(1024), 512).rearrange("p (t n) -> p t n", t=4)
            At = ab(alloc(1024), 512).rearrange("p (t n) -> p t n", t=4)
            o_sq = alloc(2048)
            sq = ab(o_sq, 1024).rearrange("p (c n) -> p c n", c=2)
            rs2 = af(alloc(2048), 512)
            t1 = af(o_sq, 512)
            Sf = [af(alloc(1024), 256) for _ in range(2)]
            Sbf = ab(alloc(2560), 1280).rearrange("p (t n) -> p t n", t=5)
            S0q = [af(alloc(4096), 1024).rearrange("p (b n) -> p b n", b=4) for _ in range(3)]
            Vp = [ab(o_sq + i * 1024, 512) for i in range(2)]
            tmpk = af(alloc(512), 128)
            assert o[0] <= ARENA_BYTES, o[0]
            D_ = [ps[0], ps[1]]
            O_ = [ps[4], ps[5]]

            heads = {}

            def nitem(si):
                def f():
                    t0, n = STS[si]
                    if si == 0:
                        dma("pool", WAL[:, :, :], w_in[j].rearrange("(k p) n -> p k n", p=128)[:, :, 3072:3088])
                        dma("pool", WA[0:16, :], w_a2[j])
                        dma("pool", WA[16:17, :], b_a[j:j + 1, :])
                    if si == 0:
                        memset("pool", AL[0:32, :], 1.0)
                    rs = norm_stats(si, ps[6 + (si % 2)])
                    norm_apply(si, rs, V_MIX + l * 8)

                    def alow(sj):
                        tj, nj = STS[sj]
                        pb = ps[6 + ((sj + 1) % 2)]
                        for k in range(NCH):
                            mm(pb[0:16, 0:nj], WAL[:, k, :], hT[:, k, tj:tj + nj], start=(k == 0), stop=(k == NCH - 1))
                        cp("act", AL[0:16, tj:tj + nj], pb[0:16, 0:nj])
                    if si > 0:
                        alow(si - 1)
                    if si == len(STS) - 1:
                        alow(si)
                return f
            nitems = [nitem(si) for si in range(len(STS))]
            scale_q = 128.0 ** -0.5

            class U:
                pass
            units = []
            for h in range(4):
                for si, (t0, n) in enumerate(STS):
                    u = U()
                    u.h, u.si, u.t0, u.n, u.nt = h, si, t0, n, n // 128
                    u.sample = (si == len(STS) - 1)
                    units.append(u)

            def A_steps(u):
                h, t0, n, nt = u.h, u.t0, u.n, u.nt
                if u.si == 0 and h == 0:
                    heads[0] = load_gla_head(j, 0)
                WQ, WK, WV, WR, WO = heads[h]
                slot_t = WQ.tensor
                if u.si == 2 and h + 1 < 4:
                    heads[h + 1] = load_gla_head(j, h + 1)
                if u.si == 1:
                    for q in range(3):
                        dma("sp", S0q[q][:, :, :], sg_in[j][q * 4:(q + 1) * 4, h].rearrange("b d v -> d b v"))
                if u.si == 3:
                    S0b_ = slot_t[:, 8192:12288].rearrange("p (b n) -> p b n", b=16)
                    for q in range(3):
                        cp("act", S0b_[:, q * 4:(q + 1) * 4, :], S0q[q][:, :, :])
                tri = cst[:, C_TRI_S:C_TRI_S + 128] if u.sample else cst[:, C_TRI_P:C_TRI_P + 128]
                mask = cst[:, C_MASK_S:C_MASK_S + 128] if u.sample else cst[:, C_MASK_P:C_MASK_P + 128]
                X, Y, Q, Kb, R0, R1, A, TR, V0, V1 = ps[7], ps[3], ps[2], ps[3], ps[2], ps[3], ps[0], ps[1], ps[2], ps[3]

                def a7():
                    for t in range(nt):
                        mm(X[:, t * 128:(t + 1) * 128], AL[0:17, t0 + t * 128:t0 + (t + 1) * 128],
                           WA[0:17, h * 128:(h + 1) * 128])
                    act(e1[:, 0:n], X[:, 0:n], AF.Exp, scale=-1.0)
                    act(e1[:, 0:n], e1[:, 0:n], AF.Ln, bias=1.0)

                def a8():
                    for t in range(nt):
                        mm(Y[:, t * 128:(t + 1) * 128], e1[:, t * 128:(t + 1) * 128], tri)
                    act(Ep[:, 0:n], Y[:, 0:n], AF.Exp)
                    act(Em[:, 0:n], Y[:, 0:n], AF.Exp, scale=-1.0)

                def a1():
                    for k in range(NCH):
                        mm(Q[:, 0:n], WQ[:, k, :], hT[:, k, t0:t0 + n], start=(k == 0), stop=(k == NCH - 1))
                    cp("act", qf[:, 0:n], Q[:, 0:n])

                def a2():
                    for k in range(NCH):
                        mm(Kb[:, 0:n], WK[:, k, :], hT[:, k, t0:t0 + n], start=(k == 0), stop=(k == NCH - 1))
                    cp("act", kf[:, 0:n], Kb[:, 0:n])

                def a9():
                    stt("dve", qt[:, 0:n], qf[:, 0:n], scale_q, Ep[:, 0:n], ALU.mult, ALU.mult)
                    tt("dve", kt[:, 0:n], kf[:, 0:n], Em[:, 0:n], ALU.mult)
                    if not u.sample:
                        for t in range(nt):
                            sl = slice(t * 128, (t + 1) * 128)
                            stt("dve", kh[:, sl], kf[:, sl], Ep[:, t * 128 + 127:t * 128 + 128], Em[:, sl],
                                ALU.mult, ALU.mult)
                    else:
                        ebrow = Ep[:, 0:128].rearrange("p (b t) -> p b t", b=16)[:, :, 7:8].broadcast_to([128, 16, 8])
                        tt("dve", tmpk[:, :].rearrange("p (b t) -> p b t", b=16),
                           Em[:, 0:128].rearrange("p (b t) -> p b t", b=16), ebrow, ALU.mult)
                        tt("dve", kh[:, 0:128], kf[:, 0:128], tmpk[:, :], ALU.mult)

                def a3():
                    for k in range(NCH):
                        mm(R0[:, 0:n], WR[:, k, 0:128], hT[:, k, t0:t0 + n], start=(k == 0), stop=(k == NCH - 1))
                    act(sr[:, 0, 0:n], R0[:, 0:n], AF.Silu)

                def a4():
                    for k in range(NCH):
                        mm(R1[:, 0:n], WR[:, k, 128:256], hT[:, k, t0:t0 + n], start=(k == 0), stop=(k == NCH - 1))
                    act(sr[:, 1, 0:n], R1[:, 0:n], AF.Silu)

                def a10():
                    for t in range(nt):
                        sl = slice(t * 128, (t + 1) * 128)
                        mm(A[:, sl], kt[:, sl], qt[:, sl])
                    tt("dve", At[:, 0:nt, :], A[:, 0:n].rearrange("p (t n) -> p t n", t=nt),
                       mask.unsqueeze(1).broadcast_to([128, nt, 128]), ALU.mult)

                def a11():
                    for t in range(nt):
                        sl = slice(t * 128, (t + 1) * 128)
                        mm(TR[:, sl], kh[:, sl], idb[:, :])
                    cp("act", khtok[:, 0:nt, :], TR[:, 0:n].rearrange("p (t n) -> p t n", t=nt))

                def vproj(bank, tiles):
                    for i, t in enumerate(tiles):
                        for k in range(NCH):
                            mm(bank[:, i * 256:(i + 1) * 256], hT[:, k, t0 + t * 128:t0 + (t + 1) * 128], WV[:, k, :],
                               start=(k == 0), stop=(k == NCH - 1))
                    nn = len(tiles)
                    cp("dve", vt[:, tiles[0]:tiles[0] + nn, :], bank[:, 0:nn * 256].rearrange("p (t n) -> p t n", t=nn))

                def a5():
                    vproj(V0, list(range(min(2, nt))))

                def a6():
                    if nt > 2:
                        vproj(V1, list(range(2, nt)))

                def a12():
                    if not u.sample:
                        for t in range(nt):
                            mm(D_[t // 2][:, (t % 2) * 256:(t % 2 + 1) * 256], khtok[:, t, :], vt[:, t, :])
                    else:
                        S0b = slot_t[:, 8192:12288].rearrange("p (b n) -> p b n", b=16)
                        u.S0b = S0b
                        banks = [ps[0], ps[1], ps[4], ps[5]]

                        def pair_mm(pi):
                            q, p = pi // 2, pi % 2
                            b0 = q * 4 + 2 * p
                            vp = Vp[pi % 2]
                            for i in range(2):
                                act(vp[:, i * 256:(i + 1) * 256], vt[:, 0, :], AF.Copy,
                                    scale=cst[:, C_RMASK + b0 + i:C_RMASK + b0 + i + 1])
                            mm(banks[pi % 4][:, 0:512], khtok[:, 0, :], vp[:, :])

                        def pair_upd(pi):
                            q, p = pi // 2, pi % 2
                            Sq = S0q[q % 3]
                            Dp = banks[pi % 4]
                            for i in range(2):
                                b = q * 4 + 2 * p + i
                                stt("dve", Sq[:, 2 * p + i, :], Sq[:, 2 * p + i, :], Ep[:, b * 8 + 7:b * 8 + 8],
                                    Dp[:, i * 256:(i + 1) * 256], ALU.mult, ALU.add)
                            if p == 1:
                                dma("sp", ngs[j][q * 4:(q + 1) * 4, h].rearrange("b d v -> d b v"), Sq[:, :, :])
                                if q == 0:
                                    dma("sp", S0q[0][:, :, :], sg_in[j][12:16, h].rearrange("b d v -> d b v"))
                                    cp("act", S0b[:, 12:16, :], S0q[0][:, :, :])
                        for pi in range(8):
                            pair_mm(pi)
                            if pi >= 2:
                                pair_upd(pi - 2)
                        pair_upd(6)
                        pair_upd(7)
                return dict(a7=a7, a8=a8, a1=a1, a2=a2, a9=a9, a3=a3, a4=a4, a10=a10, a11=a11, a5=a5, a6=a6, a12=a12)

            def B_steps(u):
                h, t0, n, nt = u.h, u.t0, u.n, u.nt
                WQ, WK, WV, WR, WO = heads[h]
                T = ps[6]
                tg0 = t0 // 128

                def b1():
                    if u.sample:
                        return
                    for t in range(nt):
                        tg = tg0 + t
                        Dt = D_[t // 2][:, (t % 2) * 256:(t % 2 + 1) * 256]
                        if tg == 0:
                            cp("dve", Sf[0][:, :], Dt)
                        else:
                            stt("dve", Sf[tg % 2][:, :], Sf[(tg - 1) % 2][:, :], Ep[:, t * 128 + 127:t * 128 + 128], Dt,
                                ALU.mult, ALU.add)
                        cp("act", Sbf[:, tg % 5, :], Sf[tg % 2][:, :])
                        if tg == NTILE - 2:
                            dma("sp", ngp[j, h], Sf[tg % 2][:, :])

                def b2():
                    for c in range(2):
                        cs = slice(c * 128, (c + 1) * 128)
                        if not u.sample:
                            for t in range(nt):
                                tg = tg0 + t
                                sl = slice(t * 128, (t + 1) * 128)
                                mm(O_[c][:, sl], vt[:, t, cs], At[:, t, :], start=True, stop=(tg == 0))
                                if tg > 0:
                                    mm(O_[c][:, sl], Sbf[:, (tg - 1) % 5, cs], qt[:, sl], start=False, stop=True)
                        else:
                            mm(O_[c][:, 0:128], vt[:, 0, cs], At[:, 0, :], start=True, stop=False)
                            for b in range(NSS):
                                mm(O_[c][:, b * 8:b * 8 + 8], u.S0b[:, b, cs], qt[:, b * 8:b * 8 + 8],
                                   start=False, stop=(b == NSS - 1))

                def b3a():
                    for c in range(2):
                        act(sq[:, c, 0:n], O_[c][:, 0:n], AF.Square, scale=1.0 / 16.0)

                def b3b():
                    for c in range(2):
                        mm(T[:, 0:n], onesb[:, :], sq[:, c, 0:n], start=(c == 0), stop=(c == 1))
                    act(rs2[:, 0:n], T[:, 0:n], AF.Ln, bias=EPS)
                    act(rs2[:, 0:n], rs2[:, 0:n], AF.Exp, scale=-0.5)

                def b4():
                    for c in range(2):
                        stt("dve", t1[:, 0:n], sr[:, c, 0:n], vcol(V_GNW + j * 8, h * 2 + c), rs2[:, 0:n],
                            ALU.mult, ALU.mult)
                        tt("dve", og[:, c, 0:n], O_[c][:, 0:n], t1[:, 0:n], ALU.mult)

                def wo(ms):
                    for m in ms:
                        py = ps[(7, 6, 4, 5)[m % 4]]
                        for c in range(2):
                            mm(py[:, 0:n], WO[:, c, m * 128:(m + 1) * 128], og[:, c, 0:n], start=(c == 0), stop=(c == 1))
                        tt("dve", xT[:, m, t0:t0 + n], py[:, 0:n], xT[:, m, t0:t0 + n], ALU.add)

                def b5():
                    wo(range(0, 4))

                def b6():
                    wo(range(4, 8))
                return dict(b1=b1, b2=b2, b3a=b3a, b3b=b3b, b4=b4, b5=b5, b6=b6)

            order = ["b1", "a8", "b2", "b3a", "a1", "b3b", "a2", "a9", "b4", "a7n", "a3", "a4", "a10", "b5", "a11", "a5",
                     "b6", "a6", "a12"]
            def giter(i):
                def f():
                    if i == 0:
                        acache[0] = A_steps(units[0])
                        acache[0]["a7"]()
                    A = acache.pop(i) if i < len(units) else {}
                    B = B_steps(units[i - 1]) if i >= 1 else {}
                    for name in order:
                        if name == "a7n":
                            if i + 1 < len(units):
                                acache[i + 1] = A_steps(units[i + 1])
                                acache[i + 1]["a7"]()
                            continue
                        fn = A.get(name) or B.get(name)
                        if fn is not None:
                            fn()
                return f
            acache = {}
            return nitems, [giter(i) for i in range(len(units) + 1)]

        def final_items():
            yb = [af(PB0 + i * 2048, 512) for i in range(2)]
            stg = [af(PB0 + 4096 + i * 4096, 1024) for i in range(2)]
            kk = {"n": 0}

            def mk(si):
                def f():
                    t0, n = STS[si]
                    rs = norm_stats(si, ps[6 + (si % 2)])
                    for tl in range(n // 128):
                        tg = t0 // 128 + tl
                        st = stg[tg % 2]
                        for half in range(2):
                            pb = ps[(2 * tg + half) % 4]
                            for cc in range(4):
                                c = half * 4 + cc
                                y = yb[kk["n"] % 2]
                                kk["n"] += 1
                                stt("dve", y[:, 0:128], xT[:, c, t0 + tl * 128:t0 + (tl + 1) * 128], vcol(V_FIN, c),
                                    rs[:, tl * 128:(tl + 1) * 128], ALU.mult, ALU.mult)
                                mm(pb[:, cc * 128:(cc + 1) * 128], y[:, 0:128], ident)
                            cp("act", st[:, half * 512:(half + 1) * 512], pb[:, :])
                        dma("sp", y_out[tg * 128:(tg + 1) * 128, :], st)
                return f
            return [mk(si) for si in range(len(STS))]

        push_stage([input_item(si) for si in range(len(STS))])
        for l in range(4):
            if l % 2 == 0:
                push_stage(pool_items(l))
            else:
                nitems, gitems = gla_layer(l)
                push_stage(nitems)
                gbase = push_stage(gitems)
                push_stage(ffn_norm_items(l), base=gbase + 33.0)
                keyc["base"] = gbase + 2.0 * len(gitems)
            for gi_, (c0, g) in enumerate(FFN_GROUPS):
                push_stage(ffn_group_items(l, c0, g, with_norm=(l % 2 == 0 and gi_ == 0)))
        push_stage(final_items())
        sched.sort(key=lambda t: (t[0], t[1]))
        for _, _, f in sched:
            f()

        P.finalize(nc, stack)
        with nc.Block() as block:
            @block.tensor
            def _(e):
                P.emit("pe", e)

            @block.scalar
            def _(e):
                P.emit("act", e)

            @block.vector
            def _(e):
                P.emit("dve", e)

            @block.gpsimd
            def _(e):
                P.emit("pool", e)

            @block.sync
            def _(e):
                P.emit("sp", e, final=True)
    return nc, P


def _make_consts():
    c = np.zeros((128, NCONST), np.float32)
    c[:, C_ID:C_ID + 128] = np.eye(128, dtype=np.float32)
    j = np.arange(128)[:, None]
    i = np.arange(128)[None, :]
    causal = (j <= i)
    same = (j // TS) == (i // TS)
    c[:, C_TRI_P:C_TRI_P + 128] = np.where(causal, -1.0 / 16.0, 0.0)
    c[:, C_TRI_S:C_TRI_S + 128] = np.where(causal & same, -1.0 / 16.0, 0.0)
    c[:, C_MASK_P:C_MASK_P + 128] = causal.astype(np.float32)
    c[:, C_MASK_S:C_MASK_S + 128] = (causal & same).astype(np.float32)
    c[:, C_RMASK:C_RMASK + 16] = ((np.arange(128)[:, None] // TS) == np.arange(16)[None, :]).astype(np.float32)
    for g in range(4):
        w = 2 << g
        t = np.arange(15)
        c[:, C_IC + g * 15:C_IC + (g + 1) * 15] = (1.0 / np.minimum(t + 1, w)).astype(np.float32)[None, :]
        c[:, C_IC2 + g * 15:C_IC2 + (g + 1) * 15] = (float(w) / np.minimum(t + 1, w)).astype(np.float32)[None, :]
    return c


def _colvec(v):
    v = np.asarray(v, np.float32).reshape(-1, NCH, 128)
    return np.ascontiguousarray(v.transpose(2, 0, 1).reshape(128, -1))


_CACHE = {}


def kernel(x_prompt, x_sample, state_pool, state_gla, norm_mix, norm_ffn, norm_final, pool_w, pool_scale,
           gla_w_in, gla_w_a2, gla_b_a, gla_norm, gla_w_o, ffn_w1, ffn_w3, ffn_w2):
    f = lambda a: np.ascontiguousarray(np.asarray(a, dtype=np.float32))
    x_prompt, x_sample, state_pool, state_gla = f(x_prompt), f(x_sample), f(state_pool), f(state_gla)
    if "nc" not in _CACHE:
        _CACHE["nc"] = build_program()
    nc, _ = _CACHE["nc"]
    vecs = np.concatenate([_colvec(norm_mix), _colvec(norm_ffn), _colvec(norm_final), _colvec(pool_scale),
                           _colvec(gla_norm)], axis=1)
    assert vecs.shape == (128, NVEC)
    consts = _make_consts()
    shared = {
        "vecs": vecs, "consts": consts, "pool_w": f(pool_w), "gla_w_in": f(gla_w_in), "gla_w_a2": f(gla_w_a2),
        "gla_b_a": f(gla_b_a), "gla_w_o": f(gla_w_o), "ffn_w1": f(ffn_w1), "ffn_w3": f(ffn_w3), "ffn_w2": f(ffn_w2),
    }
    in_maps = []
    for c in range(NC8):
        m = dict(shared)
        m["xin"] = np.ascontiguousarray(
            np.concatenate([x_prompt[c], x_sample[c * NSS:(c + 1) * NSS].reshape(NSS * TS, D)], axis=0))
        m["sp_in"] = np.ascontiguousarray(state_pool[:, c * NSS:(c + 1) * NSS])
        m["sg_in"] = np.ascontiguousarray(state_gla[:, c * NSS:(c + 1) * NSS])
        in_maps.append(m)
    res = run_bass_kernel_spmd(nc, in_maps, core_ids=list(range(NC8)))
    R = res.results
    y_prompt = np.stack([R[c]["y"][:SEQ] for c in range(NC8)], axis=0)
    y_sample = np.concatenate([R[c]["y"][SEQ:].reshape(NSS, TS, D) for c in range(NC8)], axis=0)
    new_pool_prompt = np.stack([R[c]["npp"] for c in range(NC8)], axis=1)
    new_gla_prompt = np.stack([R[c]["ngp"] for c in range(NC8)], axis=1)
    new_pool_sample = np.concatenate([R[c]["nps"] for c in range(NC8)], axis=1)
    new_gla_sample = np.concatenate([R[c]["ngs"] for c in range(NC8)], axis=1)
    return (y_prompt.astype(np.float32), y_sample.astype(np.float32), new_pool_prompt.astype(np.float32),
            new_gla_prompt.astype(np.float32), new_pool_sample.astype(np.float32), new_gla_sample.astype(np.float32))
```
